# Optimizing a Trainium2 kernel written in Bass

```python
import math
import jax, jax.numpy as jnp
from jax import lax
import numpy as np

D_MODEL = 1024
BATCH = 4
SEQ = 8192
DEPTH = 2

GRID_W = 64
CTX_LEN = 256
N_MIXERS = 2
N_MOD = 6
NORM_EPS = 1e-6

NA_HEADS = 16
NA_HEAD_DIM = D_MODEL // NA_HEADS
NA_WIN_H = 8
NA_WIN_W = 16

HY_FILTER_WIDTH = 64
HY_EMB_DIM = 33
HY_BANDS = (HY_EMB_DIM - 1) // 2
HY_SHORT_WIDTH = 3
HY_DECAY_TARGET = 1e-2
HY_SHORT_DECAY_PCT = 0.3
HY_LONG_DECAY_PCT = 1.5

D_FF = 2816
FFN_CONV_WIDTH = 3

kernel_name = 'hybrid_na_hyena_diffusion_block'


def rms_norm(x, gain):
    xf = x.astype(jnp.float32)
    y = xf * lax.rsqrt(jnp.mean(xf * xf, axis=-1, keepdims=True) + NORM_EPS)
    return (y * gain.astype(jnp.float32)).astype(x.dtype)


def adaln(cond, w_mod, b_mod):
    return jnp.split(jax.nn.silu(cond) @ w_mod + b_mod, N_MOD, axis=-1)


def modulate(h, shift, scale):
    return h * (1.0 + scale) + shift


def dwconv_centred(x, w, b):
    k = w.shape[0]
    L = x.shape[1]
    pad = k // 2
    xp = jnp.pad(x, ((0, 0), (pad, k - 1 - pad), (0, 0)))
    out = xp[:, 0:L] * w[0]
    for i in range(1, k):
        out = out + xp[:, i:i + L] * w[i]
    return out + b


def split_heads(t):
    return t.reshape(t.shape[0], t.shape[1], NA_HEADS, NA_HEAD_DIM)


def neighbourhood_attention(q, k, v, k_ctx, v_ctx, rpb):
    B, L, H, Dh = q.shape
    rows = L // GRID_W
    kh = min(NA_WIN_H, rows)
    kw = NA_WIN_W
    scale = Dh ** -0.5
    cols = np.arange(GRID_W)
    col_idx = (np.clip(cols - kw // 2, 0, GRID_W - kw)[:, None] + np.arange(kw)[None, :]).astype(np.int32)
    dc_idx = (col_idx - cols[:, None] + (NA_WIN_W - 1)).astype(np.int32)
    qg = jnp.moveaxis(q.reshape(B, rows, GRID_W, H, Dh), 1, 0)
    kg = k.reshape(B, rows, GRID_W, H, Dh)
    vg = v.reshape(B, rows, GRID_W, H, Dh)

    def one_row(args):
        r, q_row = args
        r0 = jnp.clip(r - kh // 2, 0, rows - kh)
        k_win = lax.dynamic_slice_in_dim(kg, r0, kh, axis=1)[:, :, col_idx]
        v_win = lax.dynamic_slice_in_dim(vg, r0, kh, axis=1)[:, :, col_idx]
        dr_idx = r0 + jnp.arange(kh, dtype=jnp.int32) - r + (NA_WIN_H - 1)
        bias = rpb[:, dr_idx[:, None, None], dc_idx[None, :, :]]
        s_loc = jnp.einsum('bqhd,biqjhd->bhqij', q_row, k_win) * scale + jnp.transpose(bias, (0, 2, 1, 3))[None]
        s_ctx = jnp.einsum('bqhd,bkhd->bhqk', q_row, k_ctx) * scale
        s = jnp.concatenate([s_loc.reshape(B, H, GRID_W, kh * kw), s_ctx], axis=-1)
        p = jax.nn.softmax(s.astype(jnp.float32), axis=-1).astype(v.dtype)
        p_loc = p[..., :kh * kw].reshape(B, H, GRID_W, kh, kw)
        return (jnp.einsum('bhqij,biqjhd->bqhd', p_loc, v_win)
                + jnp.einsum('bhqk,bkhd->bqhd', p[..., kh * kw:], v_ctx))

    out = lax.map(one_row, (jnp.arange(rows, dtype=jnp.int32), qg))
    return jnp.moveaxis(out, 0, 1).reshape(B, L, H * Dh)


def context_attention(q, k, v):
    s = jnp.einsum('bqhd,bkhd->bhqk', q, k) * (q.shape[-1] ** -0.5)
    p = jax.nn.softmax(s.astype(jnp.float32), axis=-1).astype(v.dtype)
    o = jnp.einsum('bhqk,bkhd->bqhd', p, v)
    return o.reshape(o.shape[0], o.shape[1], -1)


def hyena_filters(L, w1, b1, w2, b2, w3, b3, w_out, sin_freq):
    f32 = jnp.float32
    t = jnp.linspace(0.0, 1.0, L, dtype=f32)[:, None]
    w = (2.0 * math.pi / L) * jnp.arange(L, dtype=f32)[:, None]
    bands = jnp.linspace(1e-4, HY_BANDS - 1, HY_BANDS, dtype=f32)[None, :]
    z = jnp.concatenate([t, jnp.cos(bands * w), -jnp.sin(bands * w)], axis=-1)
    freq = sin_freq.astype(f32)
    hdn = jnp.sin(freq * (z @ w1.astype(f32) + b1.astype(f32)))
    hdn = jnp.sin(freq * (hdn @ w2.astype(f32) + b2.astype(f32)))
    hdn = jnp.sin(freq * (hdn @ w3.astype(f32) + b3.astype(f32)))
    h = (hdn @ w_out.astype(f32)).reshape(L, 2, D_MODEL)
    deltas = jnp.abs(jnp.linspace(math.log(HY_DECAY_TARGET) / HY_SHORT_DECAY_PCT,
                                  math.log(HY_DECAY_TARGET) / HY_LONG_DECAY_PCT, D_MODEL, dtype=f32))
    h = h * jnp.exp(-t * deltas)[:, None, :]
    h = h * lax.rsqrt(jnp.sum(h * h, axis=(0, 1), keepdims=True) + NORM_EPS)
    return jnp.concatenate([h[:, 0], jnp.zeros((1, D_MODEL), f32), h[:0:-1, 1]], axis=0)


def hyena_mixer(h, w_in, b_in, short_w, short_b, f_w1, f_b1, f_w2, f_b2, f_w3, f_b3, f_wout, f_freq,
                d_bias, w_out, b_out):
    L = h.shape[1]
    u = dwconv_centred(h @ w_in + b_in, short_w, short_b)
    x0, x1, v = jnp.split(u, 3, axis=-1)
    filt = hyena_filters(L, f_w1, f_b1, f_w2, f_b2, f_w3, f_b3, f_wout, f_freq)
    z = (v * x1).astype(jnp.float32)
    zf = jnp.fft.rfft(z, n=2 * L, axis=1)
    ff = jnp.fft.rfft(filt, n=2 * L, axis=0)
    y = jnp.fft.irfft(zf * ff[None], n=2 * L, axis=1)[:, :L] + z * d_bias.astype(jnp.float32)
    return (x0 * y.astype(h.dtype)) @ w_out + b_out


def conv_ffn(h, w_up, conv_w, conv_b, w_down):
    a, g = jnp.split(h @ w_up, 2, axis=-1)
    return (a * jax.nn.gelu(dwconv_centred(g, conv_w, conv_b), approximate=False)) @ w_down


def setup_inputs(seed: int = 0) -> dict:
    key = jax.random.key(seed)
    ks = iter(jax.random.split(key, 48))

    def nrm(shape, scale):
        return scale * jax.random.normal(next(ks), shape, jnp.float32)

    D = D_MODEL
    inp = {}
    inp['x'] = nrm((BATCH, SEQ, D), 1.0)
    inp['c'] = nrm((BATCH, D), 1.0)
    inp['ctx'] = nrm((BATCH, CTX_LEN, D), 1.0)
    inp['c_ctx'] = nrm((D,), 1.0)
    inp['l0_w_mod'] = nrm((D, N_MOD * D), D ** -0.5)
    inp['l0_b_mod'] = nrm((N_MOD * D,), 0.02)
    inp['l0_norm1'] = 1.0 + nrm((D,), 0.05)
    inp['l0_norm2'] = 1.0 + nrm((D,), 0.05)
    inp['l0_na_w_qkv'] = nrm((D, 3 * D), D ** -0.5)
    inp['l0_na_q_gain'] = 1.0 + nrm((NA_HEAD_DIM,), 0.05)
    inp['l0_na_k_gain'] = 1.0 + nrm((NA_HEAD_DIM,), 0.05)
    inp['l0_na_rpb'] = nrm((NA_HEADS, 2 * NA_WIN_H - 1, 2 * NA_WIN_W - 1), 0.1)
    inp['l0_na_w_o'] = nrm((D, D), D ** -0.5)
    inp['l0_ffn_w_up'] = nrm((D, 2 * D_FF), D ** -0.5)
    inp['l0_ffn_conv_w'] = nrm((FFN_CONV_WIDTH, D_FF), FFN_CONV_WIDTH ** -0.5)
    inp['l0_ffn_conv_b'] = nrm((D_FF,), 0.02)
    inp['l0_ffn_w_down'] = nrm((D_FF, D), D_FF ** -0.5)
    inp['l1_w_mod'] = nrm((D, N_MOD * D), D ** -0.5)
    inp['l1_b_mod'] = nrm((N_MOD * D,), 0.02)
    inp['l1_norm1'] = 1.0 + nrm((D,), 0.05)
    inp['l1_norm2'] = 1.0 + nrm((D,), 0.05)
    inp['l1_hy_w_in'] = nrm((D, 3 * D), D ** -0.5)
    inp['l1_hy_b_in'] = nrm((3 * D,), 0.02)
    inp['l1_hy_short_w'] = nrm((HY_SHORT_WIDTH, 3 * D), HY_SHORT_WIDTH ** -0.5)
    inp['l1_hy_short_b'] = nrm((3 * D,), 0.02)
    inp['l1_hy_f_w1'] = nrm((HY_EMB_DIM, HY_FILTER_WIDTH), HY_EMB_DIM ** -0.5)
    inp['l1_hy_f_b1'] = nrm((HY_FILTER_WIDTH,), 0.1)
    inp['l1_hy_f_w2'] = nrm((HY_FILTER_WIDTH, HY_FILTER_WIDTH), HY_FILTER_WIDTH ** -0.5)
    inp['l1_hy_f_b2'] = nrm((HY_FILTER_WIDTH,), 0.1)
    inp['l1_hy_f_w3'] = nrm((HY_FILTER_WIDTH, HY_FILTER_WIDTH), HY_FILTER_WIDTH ** -0.5)
    inp['l1_hy_f_b3'] = nrm((HY_FILTER_WIDTH,), 0.1)
    inp['l1_hy_f_wout'] = nrm((HY_FILTER_WIDTH, 2 * D), HY_FILTER_WIDTH ** -0.5)
    inp['l1_hy_f_freq'] = 1.0 + nrm((HY_FILTER_WIDTH,), 0.05)
    inp['l1_hy_d_bias'] = nrm((D,), 1.0)
    inp['l1_hy_w_out'] = nrm((D, D), D ** -0.5)
    inp['l1_hy_b_out'] = nrm((D,), 0.02)
    inp['l1_ffn_w_up'] = nrm((D, 2 * D_FF), D ** -0.5)
    inp['l1_ffn_conv_w'] = nrm((FFN_CONV_WIDTH, D_FF), FFN_CONV_WIDTH ** -0.5)
    inp['l1_ffn_conv_b'] = nrm((D_FF,), 0.02)
    inp['l1_ffn_w_down'] = nrm((D_FF, D), D_FF ** -0.5)
    return inp


def reference(x, c, ctx, c_ctx,
              l0_w_mod, l0_b_mod, l0_norm1, l0_norm2, l0_na_w_qkv, l0_na_q_gain, l0_na_k_gain, l0_na_rpb,
              l0_na_w_o, l0_ffn_w_up, l0_ffn_conv_w, l0_ffn_conv_b, l0_ffn_w_down,
              l1_w_mod, l1_b_mod, l1_norm1, l1_norm2, l1_hy_w_in, l1_hy_b_in, l1_hy_short_w, l1_hy_short_b,
              l1_hy_f_w1, l1_hy_f_b1, l1_hy_f_w2, l1_hy_f_b2, l1_hy_f_w3, l1_hy_f_b3, l1_hy_f_wout, l1_hy_f_freq,
              l1_hy_d_bias, l1_hy_w_out, l1_hy_b_out, l1_ffn_w_up, l1_ffn_conv_w, l1_ffn_conv_b, l1_ffn_w_down):
    layers = [
        dict(w_mod=l0_w_mod, b_mod=l0_b_mod, norm1=l0_norm1, norm2=l0_norm2,
             mixer=(l0_na_w_qkv, l0_na_q_gain, l0_na_k_gain, l0_na_rpb, l0_na_w_o),
             ffn=(l0_ffn_w_up, l0_ffn_conv_w, l0_ffn_conv_b, l0_ffn_w_down)),
        dict(w_mod=l1_w_mod, b_mod=l1_b_mod, norm1=l1_norm1, norm2=l1_norm2,
             mixer=(l1_hy_w_in, l1_hy_b_in, l1_hy_short_w, l1_hy_short_b, l1_hy_f_w1, l1_hy_f_b1,
                    l1_hy_f_w2, l1_hy_f_b2, l1_hy_f_w3, l1_hy_f_b3, l1_hy_f_wout, l1_hy_f_freq,
                    l1_hy_d_bias, l1_hy_w_out, l1_hy_b_out),
             ffn=(l1_ffn_w_up, l1_ffn_conv_w, l1_ffn_conv_b, l1_ffn_w_down)),
    ]
    xc = ctx
    for i in range(DEPTH):
        p = layers[i]
        ctx_read_later = any(j % N_MIXERS == 0 for j in range(i + 1, DEPTH))
        sh1, sc1, g1, sh2, sc2, g2 = adaln(c[:, None, :], p['w_mod'], p['b_mod'])
        csh1, csc1, cg1, csh2, csc2, cg2 = adaln(c_ctx[None, None, :], p['w_mod'], p['b_mod'])
        h = modulate(rms_norm(x, p['norm1']), sh1, sc1)
        hc = modulate(rms_norm(xc, p['norm1']), csh1, csc1)
        if i % N_MIXERS == 0:
            w_qkv, q_gain, k_gain, rpb, w_o = p['mixer']
            w_q, w_k, w_v = jnp.split(w_qkv, 3, axis=1)
            q = rms_norm(split_heads(h @ w_q), q_gain)
            k = rms_norm(split_heads(h @ w_k), k_gain)
            v = split_heads(h @ w_v)
            kc = rms_norm(split_heads(hc @ w_k), k_gain)
            vc = split_heads(hc @ w_v)
            y = neighbourhood_attention(q, k, v, kc, vc, rpb) @ w_o
            if ctx_read_later:
                qc = rms_norm(split_heads(hc @ w_q), q_gain)
                yc = context_attention(qc, kc, vc) @ w_o
        else:
            y = hyena_mixer(h, *p['mixer'])
            if ctx_read_later:
                yc = hyena_mixer(hc, *p['mixer'])
        x = x + g1 * y
        x = x + g2 * conv_ffn(modulate(rms_norm(x, p['norm2']), sh2, sc2), *p['ffn'])
        if ctx_read_later:
            xc = xc + cg1 * yc
            xc = xc + cg2 * conv_ffn(modulate(rms_norm(xc, p['norm2']), csh2, csc2), *p['ffn'])
    return x
```

```python
import contextlib
import math
import numpy as np
import concourse.bass as bass
import concourse.mybir as mybir
from concourse.bass_utils import run_bass_kernel_spmd

F32 = mybir.dt.float32
BF16 = mybir.dt.bfloat16
AF = mybir.ActivationFunctionType
ALU = mybir.AluOpType
AX = mybir.AxisListType

SAME_ENG_SYNC = True


class Buf:
    __slots__ = ("name", "w", "r", "ap")

    def __init__(self, name, ap=None):
        self.name = name
        self.w = None
        self.r = []
        self.ap = ap


class Prog:
    ENGS = ["pe", "act", "dve", "pool", "sp"]

    def __init__(self, nc, n_dma_sems=12):
        self.nc = nc
        self.ops = {e: [] for e in self.ENGS}
        self.n_dma_sems = n_dma_sems
        self.stack = contextlib.ExitStack()
        self._nid = 0

    def sb(self, name, shape, dtype):
        h = self.stack.enter_context(self.nc.sbuf_tensor("s_" + name, list(shape), dtype))
        return h

    def ps(self, name, shape, dtype=F32):
        h = self.stack.enter_context(self.nc.psum_tensor("p_" + name, list(shape), dtype))
        return h

    def buf(self, name=None):
        self._nid += 1
        return Buf(name or f"b{self._nid}")

    def op(self, eng, fn, reads=(), writes=(), dma=False):
        idx = len(self.ops[eng])
        deps = set()
        for b in reads:
            if b.w is not None:
                deps.add(b.w)
        for b in writes:
            if b.w is not None:
                deps.add(b.w)
            deps.update(b.r)
        me = (eng, idx)
        deps.discard(me)
        self.ops[eng].append(dict(fn=fn, deps=deps, dma=dma))
        for b in reads:
            b.r.append(me)
        for b in writes:
            b.w = me
            b.r = []
        return me

    def dma(self, out, in_, reads=(), writes=(), eng="sp", **kw):
        return self.op(eng, lambda e: e.dma_start(out=out, in_=in_, **kw),
                       reads=reads, writes=writes, dma=True)

    def emit(self, final_wait_eng="sp"):
        nc = self.nc
        ops = self.ops
        all_dma = [(e, i) for e in self.ENGS for i, r in enumerate(ops[e]) if r["dma"]]
        ops[final_wait_eng].append(dict(fn=None, deps=set(all_dma[-64:]) if False else set(all_dma), dma=False))
        signal = {e: [False] * len(ops[e]) for e in self.ENGS}
        for e in self.ENGS:
            for i, r in enumerate(ops[e]):
                for (de, di) in r["deps"]:
                    if ops[de][di]["dma"]:
                        continue
                    if de == e and not SAME_ENG_SYNC:
                        continue
                    if de == e and e == "pe":
                        continue
                    signal[de][di] = True
        cnt = {}
        for e in self.ENGS:
            c = 0
            arr = []
            for i in range(len(ops[e])):
                if signal[e][i]:
                    c += 1
                arr.append(c)
            cnt[e] = arr
        EPOCH = 1000
        esem = {}
        for e in self.ENGS:
            tot = cnt[e][-1] if cnt[e] else 0
            nep = max(1, (tot + EPOCH - 1) // EPOCH)
            esem[e] = [self.stack.enter_context(nc.semaphore(f"S_{e}{k}")) for k in range(nep)]

        def esv(e, c):
            ep = (c - 1) // EPOCH
            return esem[e][ep], c - ep * EPOCH
        dsem = {}
        dma_info = {}
        prev_on_sem = {}
        for e in self.ENGS:
            dl = [i for i, r in enumerate(ops[e]) if r["dma"]]
            if not dl:
                continue
            n = min(self.n_dma_sems, len(dl))
            dsem[e] = [self.stack.enter_context(nc.semaphore(f"D_{e}{k}")) for k in range(n)]
            vals = [0] * n
            for k, i in enumerate(dl):
                s = k % n
                if vals[s] > 0:
                    prev_on_sem[(e, i)] = (dsem[e][s], vals[s])
                vals[s] += 16
                dma_info[(e, i)] = (dsem[e][s], vals[s])
        plan = {}
        for e in self.ENGS:
            wm_eng = {x: 0 for x in self.ENGS}
            wm_sem = {}
            pl = []
            for i, r in enumerate(ops[e]):
                waits = []
                need_eng = {}
                need_sem = {}
                if (e, i) in prev_on_sem:
                    s, v = prev_on_sem[(e, i)]
                    need_sem[s] = max(need_sem.get(s, 0), v)
                for (de, di) in r["deps"]:
                    if ops[de][di]["dma"]:
                        s, v = dma_info[(de, di)]
                        need_sem[s] = max(need_sem.get(s, 0), v)
                    else:
                        if de == e and (not SAME_ENG_SYNC or e == "pe"):
                            continue
                        need_eng[de] = max(need_eng.get(de, 0), cnt[de][di])
                for x, v in need_eng.items():
                    if v > wm_eng[x]:
                        wm_eng[x] = v
                        waits.append(esv(x, v))
                for s, v in need_sem.items():
                    if v > wm_sem.get(s, 0):
                        wm_sem[s] = v
                        waits.append((s, v))
                inc = None
                if r["dma"]:
                    inc = (dma_info[(e, i)][0], 16)
                elif signal[e][i]:
                    inc = (esv(e, cnt[e][i])[0], 1)
                pl.append((r["fn"], waits, inc))
            plan[e] = pl
        self.plan = plan
        bname = dict(pe="tensor", act="scalar", dve="vector", pool="gpsimd", sp="sync")
        with nc.Block() as block:
            for e in self.ENGS:
                def body(engobj, e=e):
                    for fn, waits, inc in plan[e]:
                        for s, v in waits:
                            engobj.wait_ge(s, v)
                        if fn is None:
                            continue
                        ins = fn(engobj)
                        if inc is not None:
                            ins.then_inc(inc[0], inc[1])
                getattr(block, bname[e])(body)
        self.stack.close()

    def stats(self):
        return {e: len(self.ops[e]) for e in self.ENGS}


D = 1024
L = 8192
NB = 4
DFF = 2816
NCORES = 8
TPC = 4096
EPS = 1e-6


def _dram(nc, name, shape, kind="ExternalInput", dtype=F32):
    return nc.dram_tensor(name, list(shape), dtype, kind=kind).ap()


class Common:
    def __init__(self, P, nc):
        self.P = P
        self.nc = nc
        self.ones = P.sb("ones", [128, 128], BF16)
        self.Bones = P.buf("ones")
        self.epsb = P.sb("epsb", [128, 1], F32)
        self.Beps = P.buf("eps")
        ones, epsb = self.ones, self.epsb
        P.op("pool", lambda e: e.memset(ones[:], 1.0 / D), writes=[self.Bones])
        P.op("pool", lambda e: e.memset(epsb[:], EPS), writes=[self.Beps])

    def mod_vectors(self, cT_d, wmod_d, bmod_d, nvec, psum, Bpsum, ncol=1):
        P = self.P
        cT = P.sb("cT", [128, 8, ncol], F32); BcT = P.buf()
        sT = P.sb("sT", [128, 8, ncol], F32); BsT = P.buf()
        bm = P.sb("bm", [128, nvec, 8], F32); Bbm = P.buf()
        mod = P.sb("mod", [128, nvec, 8, ncol], F32); Bmod = P.buf()
        P.dma(cT[:], cT_d, writes=[BcT])
        P.dma(bm[:], bmod_d, writes=[Bbm])
        P.op("act", lambda e: e.activation(sT[:], cT[:], AF.Silu), reads=[BcT], writes=[BsT])
        wts = [P.sb("wmt0", [128, 8, 256], F32)] * 2
        Bw = [P.buf()] * 2
        cnt = 0
        for v in range(nvec):
            for q in range(4):
                s = cnt % 2
                cnt += 1
                wt = wts[s]
                P.dma(wt[:], wmod_d[:, :, v * 1024 + q * 256:v * 1024 + (q + 1) * 256], writes=[Bw[s]])
                for mi in range(2):
                    m = q * 2 + mi
                    for k in range(8):
                        P.op("pe", lambda e, wt=wt, mi=mi, m=m, k=k: e.matmul(
                            psum[:, m * ncol:(m + 1) * ncol], wt[:, k, mi * 128:(mi + 1) * 128], sT[:, k, :],
                            start=(k == 0), stop=(k == 7)), reads=[Bw[s], BsT], writes=[Bpsum])
            for m in range(8):
                P.op("dve", lambda e, v=v, m=m: e.tensor_scalar(
                    mod[:, v, m, :], psum[:, m * ncol:(m + 1) * ncol], bm[:, v, m:m + 1], None, ALU.add),
                    reads=[Bpsum, Bbm], writes=[Bmod])
        return mod, Bmod

    def norm_mod(self, xt, Bxt, W, A, Bsh, BAB, sq, Bsq, psN, BpsN, rstd, Brstd, tmps, Btmps, ht, Bht, c0=0):
        P = self.P
        ones, epsb = self.ones, self.epsb
        P.op("act", lambda e: e.activation(sq[:, :, 0:W], xt[:, :, c0:c0 + W], AF.Square), reads=[Bxt], writes=[Bsq])
        for k in range(8):
            P.op("pe", lambda e, k=k: e.matmul(psN[:, 0:W], ones[:], sq[:, k, 0:W], start=(k == 0), stop=(k == 7)),
                 reads=[self.Bones, Bsq], writes=[BpsN])
        P.op("act", lambda e: e.activation(rstd[:, 0:W], psN[:, 0:W], AF.Ln, bias=epsb[:], scale=1.0),
             reads=[BpsN, self.Beps], writes=[Brstd])
        P.op("act", lambda e: e.activation(rstd[:, 0:W], rstd[:, 0:W], AF.Exp, scale=-0.5),
             reads=[Brstd], writes=[Brstd])
        for k in range(8):
            tmp, Btmp = tmps[k % 2], Btmps[k % 2]
            P.op("dve", lambda e, k=k, tmp=tmp: e.tensor_tensor(tmp[:, 0:W], xt[:, k, c0:c0 + W], rstd[:, 0:W], ALU.mult),
                 reads=[Bxt, Brstd], writes=[Btmp])
            P.op("act", lambda e, k=k, tmp=tmp: e.activation(ht[:, k, c0:c0 + W], tmp[:, 0:W], AF.Identity,
                                                             bias=Bsh(k), scale=A(k)),
                 reads=[Btmp, BAB], writes=[Bht[k]])


FW = 256
NT_FFN = TPC // FW


def build_ffn(dbg=False):
    nc = bass.Bass("TRN2", target_bir_lowering=False)
    if dbg:
        dbg_d = _dram(nc, "dbg", [128, 3, 8], kind="ExternalOutput")
        dbg2_d = _dram(nc, "dbg2", [128, 8, FW + 2], kind="ExternalOutput", dtype=BF16)
    xe = _dram(nc, "xe", [128, 8, TPC + 2])
    em_d = _dram(nc, "em", [128, 2])
    cT_d = _dram(nc, "cT", [128, 8, 1])
    wmod_d = _dram(nc, "wmod", [128, 8, 3 * D])
    bmod_d = _dram(nc, "bmod", [128, 3, 8])
    norm_d = _dram(nc, "norm", [128, 8])
    wup_d = _dram(nc, "wup", [128, 8, 2 * DFF])
    cw_d = _dram(nc, "cw", [128, 22, 3])
    cb_d = _dram(nc, "cb", [128, 22])
    wdn_d = _dram(nc, "wdn", [128, 22, D])
    out_d = _dram(nc, "out", [128, 8, TPC], kind="ExternalOutput")
    P = Prog(nc)
    C = Common(P, nc)
    W = FW + 2
    wup = P.sb("wup", [128, 8, 2 * DFF], BF16); Bwup = [P.buf() for _ in range(8)]
    wdn = P.sb("wdn", [128, 22, D], BF16); Bwdn = [P.buf() for _ in range(2)]
    for k in range(8):
        P.dma(wup[:, k, :], wup_d[:, k, :], writes=[Bwup[k]], eng="pool")
    P.dma(wdn[:, 0:11, :], wdn_d[:, 0:11, :], writes=[Bwdn[0]], eng="pool")
    P.dma(wdn[:, 11:22, :], wdn_d[:, 11:22, :], writes=[Bwdn[1]], eng="pool")
    em = P.sb("em", [128, 2], F32); Bem = P.buf()
    nrm = P.sb("nrm", [128, 8], F32); Bnrm = P.buf()
    cw = P.sb("cw", [128, 22, 3], F32); Bcw = P.buf()
    cb = P.sb("cb", [128, 22], F32); Bcb = P.buf()
    P.dma(em[:], em_d, writes=[Bem]); P.dma(nrm[:], norm_d, writes=[Bnrm])
    P.dma(cw[:], cw_d, writes=[Bcw]); P.dma(cb[:], cb_d, writes=[Bcb])
    psN = P.ps("psN", [128, 512]); BpsN = P.buf()
    psA = [P.ps(f"psA{i}", [128, 512]) for i in range(2)]; BpsA = [P.buf() for _ in range(2)]
    psG = [P.ps(f"psG{i}", [128, 512]) for i in range(2)]; BpsG = [P.buf() for _ in range(2)]
    psD = [P.ps(f"psD{i}", [128, 512]) for i in range(2)]; BpsD = [P.buf() for _ in range(2)]
    mod, Bmod = C.mod_vectors(cT_d, wmod_d, bmod_d, 3, psN, BpsN)
    AA = P.sb("AA", [128, 8], F32); BAA = P.buf()
    P.op("dve", lambda e: e.tensor_scalar(AA[:], mod[:, 1, :, 0], 1.0, None, ALU.add), reads=[Bmod], writes=[BAA])
    P.op("dve", lambda e: e.tensor_tensor(AA[:], AA[:], nrm[:], ALU.mult), reads=[BAA, Bnrm], writes=[BAA])
    if dbg:
        P.dma(dbg_d, mod[:, :, :, 0], reads=[Bmod])
    A = lambda k: AA[:, k:k + 1]
    Bsh = lambda k: mod[:, 0, k, :]
    G2 = lambda m: mod[:, 2, m, :]
    BAB = P.buf()
    junk = P.sb("junk", [128, 1], F32)
    P.op("dve", lambda e: e.tensor_copy(junk[:], AA[:, 0:1]), reads=[BAA, Bmod], writes=[BAB])
    xts = [P.sb(f"xt{i}", [128, 8, W], F32) for i in range(2)]; Bxts = [P.buf() for _ in range(2)]
    sq = P.sb("sq", [128, 8, W], BF16); Bsq = P.buf()
    rstd = P.sb("rstd", [128, W], F32); Brstd = P.buf()
    tmps = [P.sb(f"tmp{i}", [128, W], F32) for i in range(2)]; Btmps = [P.buf() for _ in range(2)]
    ht = P.sb("ht", [128, 8, W], BF16); Bht = [P.buf() for _ in range(8)]
    Gs = [P.sb(f"G{i}", [128, W], F32) for i in range(2)]; BGs = [P.buf() for _ in range(2)]
    cvs = [P.sb(f"cv{i}", [128, FW], F32) for i in range(2)]; Bcvs = [P.buf() for _ in range(2)]
    U = P.sb("U", [128, 22, FW], BF16); BU = [P.buf() for _ in range(22)]
    for t in range(NT_FFN):
        xt, Bxt = xts[t % 2], Bxts[t % 2]
        P.dma(xt[:], xe[:, :, FW * t:FW * t + W], writes=[Bxt])
        C.norm_mod(xt, Bxt, W, A, Bsh, BAB, sq, Bsq, psN, BpsN, rstd, Brstd, tmps, Btmps, ht, Bht)
        if dbg and t == 0:
            P.dma(dbg2_d, ht[:], reads=Bht)
        for j in range(22):
            pa, Bpa = psA[j % 2], BpsA[j % 2]
            pg, Bpg = psG[j % 2], BpsG[j % 2]
            G, BG = Gs[j % 2], BGs[j % 2]
            cv, Bcv = cvs[j % 2], Bcvs[j % 2]
            for k in range(8):
                P.op("pe", lambda e, k=k, j=j, pg=pg: e.matmul(pg[:, 0:W], wup[:, k, DFF + j * 128:DFF + (j + 1) * 128],
                                                             ht[:, k, :], start=(k == 0), stop=(k == 7)),
                     reads=[Bwup[k], Bht[k]], writes=[Bpg])
            for k in range(8):
                P.op("pe", lambda e, k=k, j=j, pa=pa: e.matmul(pa[:, 0:W], wup[:, k, j * 128:(j + 1) * 128],
                                                             ht[:, k, :], start=(k == 0), stop=(k == 7)),
                     reads=[Bwup[k], Bht[k]], writes=[Bpa])
            P.op("act", lambda e, G=G, pg=pg: e.activation(G[:, 0:W], pg[:, 0:W], AF.Identity), reads=[Bpg], writes=[BG])
            if t == 0:
                P.op("dve", lambda e, G=G: e.tensor_scalar(G[:, 0:1], G[:, 0:1], em[:, 0:1], None, ALU.mult),
                     reads=[BG, Bem], writes=[BG])
            if t == NT_FFN - 1:
                P.op("dve", lambda e, G=G: e.tensor_scalar(G[:, W - 1:W], G[:, W - 1:W], em[:, 1:2], None, ALU.mult),
                     reads=[BG, Bem], writes=[BG])
            P.op("dve", lambda e, G=G, cv=cv, j=j: e.tensor_scalar(cv[:], G[:, 1:FW + 1], cw[:, j, 1:2], cb[:, j:j + 1],
                                                                 ALU.mult, ALU.add), reads=[BG, Bcw, Bcb], writes=[Bcv])
            P.op("dve", lambda e, G=G, cv=cv, j=j: e.scalar_tensor_tensor(cv[:], G[:, 0:FW], cw[:, j, 0:1], cv[:],
                                                                        ALU.mult, ALU.add), reads=[BG, Bcw, Bcv], writes=[Bcv])
            P.op("dve", lambda e, G=G, cv=cv, j=j: e.scalar_tensor_tensor(cv[:], G[:, 2:FW + 2], cw[:, j, 2:3], cv[:],
                                                                        ALU.mult, ALU.add), reads=[BG, Bcw, Bcv], writes=[Bcv])
            P.op("act", lambda e, cv=cv: e.activation(cv[:], cv[:], AF.Gelu), reads=[Bcv], writes=[Bcv])
            P.op("dve", lambda e, cv=cv, pa=pa, j=j: e.tensor_tensor(U[:, j, :], pa[:, 1:FW + 1], cv[:], ALU.mult),
                 reads=[Bpa, Bcv], writes=[BU[j]])
        for m in range(8):
            pd, Bpd = psD[m % 2], BpsD[m % 2]
            for j in range(22):
                P.op("pe", lambda e, j=j, m=m, pd=pd: e.matmul(pd[:, 0:FW], wdn[:, j, m * 128:(m + 1) * 128], U[:, j, :],
                                                             start=(j == 0), stop=(j == 21)),
                     reads=[Bwdn[j // 11], BU[j]], writes=[Bpd])
            P.op("dve", lambda e, m=m, pd=pd, xt=xt: e.scalar_tensor_tensor(xt[:, m, 1:FW + 1], pd[:, 0:FW], G2(m),
                                                                          xt[:, m, 1:FW + 1], ALU.mult, ALU.add),
                 reads=[Bpd, Bmod, Bxt], writes=[Bxt])
        P.dma(out_d[:, :, FW * t:FW * (t + 1)], xt[:, :, 1:FW + 1], reads=[Bxt])
    P.emit()
    return nc, P


def pm(w, kc=None):
    K, N = w.shape
    return np.ascontiguousarray(w.reshape(K // 128, 128, N).transpose(1, 0, 2))


def vec_pm(v):
    return np.ascontiguousarray(v.reshape(-1, 128).T)


def feat_major(xb):
    T, Dd = xb.shape
    return np.ascontiguousarray(xb.T.reshape(Dd // 128, 128, T).transpose(1, 0, 2))


def from_feat_major(a):
    p, kc, T = a.shape
    return np.ascontiguousarray(a.transpose(1, 0, 2).reshape(kc * 128, T).T)


_NC_CACHE = {}


def _get_nc(name, builder):
    if name not in _NC_CACHE:
        _NC_CACHE[name] = builder()
    return _NC_CACHE[name]


def _ext_cols(xb_fm, t0, n, halo):
    out = np.zeros((128, xb_fm.shape[1], n + 2 * halo), np.float32)
    lo, hi = t0 - halo, t0 + n + halo
    slo, shi = max(lo, 0), min(hi, xb_fm.shape[2])
    out[:, :, slo - lo:shi - lo] = xb_fm[:, :, slo:shi]
    return out


def run_ffn(xin, c, w_mod, b_mod, norm2, w_up, conv_w, conv_b, w_down):
    nc, _ = _get_nc("ffn", build_ffn)
    shared = dict(
        wmod=pm(w_mod[:, 3 * D:6 * D]),
        bmod=np.ascontiguousarray(b_mod[3 * D:6 * D].reshape(3, 8, 128).transpose(2, 0, 1)),
        norm=vec_pm(norm2),
        wup=pm(w_up),
        cw=np.ascontiguousarray(conv_w.reshape(3, 22, 128).transpose(2, 1, 0)),
        cb=vec_pm(conv_b),
        wdn=pm(w_down),
    )
    in_maps = []
    for core in range(NCORES):
        b, hf = core // 2, core % 2
        t0 = hf * TPC
        xfm = feat_major(xin[b])
        em = np.zeros((128, 2), np.float32)
        em[:, 0] = 1.0 if t0 - 1 >= 0 else 0.0
        em[:, 1] = 1.0 if t0 + TPC < L else 0.0
        m = dict(shared)
        m.update(xe=_ext_cols(xfm, t0, TPC, 1), em=em, cT=vec_pm(c[b])[:, :, None].copy())
        in_maps.append(m)
    res = run_bass_kernel_spmd(nc, in_maps, core_ids=list(range(NCORES)))
    out = np.empty_like(xin)
    for core in range(NCORES):
        b, hf = core // 2, core % 2
        out[b, hf * TPC:(hf + 1) * TPC] = from_feat_major(res.results[core]["out"])
    return out


NFFT = 2 * L
TWO_PI = 2.0 * math.pi if False else 6.283185307179586
PI = 3.141592653589793
GS = 4


def build_conv(stage=9, use_pool=True):
    nc = bass.Bass("TRN2", target_bir_lowering=False)
    zin_d = _dram(nc, "zin", [64, 512, 128])
    zf_d = _dram(nc, "zf", [33, 2, L])
    w1_d = _dram(nc, "w1", [33, 64])
    w23_d = _dram(nc, "w23", [64, 2, 64])
    bfr_d = _dram(nc, "bfr", [64, 4])
    wout_d = _dram(nc, "wout", [64, 2, 128])
    decay_d = _dram(nc, "decay", [128, 32, 128, 4])
    dbias_d = _dram(nc, "dbias", [1, 128])
    con_d = _dram(nc, "con", [128, 4, 128])
    tw_d = _dram(nc, "tw", [128, 2, 128])
    y_d = _dram(nc, "y", [64, 512, 128], kind="ExternalOutput")
    P = Prog(nc)
    con = P.sb("con", [128, 4, 128], BF16); Bcon = P.buf()
    P.dma(con[:], con_d, writes=[Bcon], eng="pool")
    tw = P.sb("tw", [128, 2, 128], F32); Btw = P.buf()
    P.dma(tw[:], tw_d, writes=[Btw])
    ones32 = P.sb("ones32", [128, 128], F32); Bones = P.buf()
    P.op("pool", lambda e: e.memset(ones32[:], 1.0), writes=[Bones])
    epsb = P.sb("epsb", [128, 1], F32); Beps = P.buf()
    P.op("pool", lambda e: e.memset(epsb[:], EPS), writes=[Beps])
    negpi = P.sb("negpi", [128, 1], F32); Bnegpi = P.buf()
    P.op("pool", lambda e: e.memset(negpi[:], -PI), writes=[Bnegpi])
    w1 = P.sb("w1", [33, 64], F32); w23 = P.sb("w23", [64, 2, 64], F32); bfr = P.sb("bfr", [64, 4], F32)
    wout = P.sb("wout", [64, 2, 128], BF16); dbias = P.sb("dbias", [1, 128], F32)
    Bw = P.buf()
    P.dma(w1[:], w1_d, writes=[Bw]); P.dma(w23[:], w23_d, writes=[Bw]); P.dma(bfr[:], bfr_d, writes=[Bw])
    P.dma(dbias[:], dbias_d, writes=[Bw])
    Bwout = P.buf()
    P.dma(wout[:], wout_d, writes=[Bwout], eng="pool")
    psA = P.ps("psA", [128, GS, 2, 128]); BpsA = P.buf()
    psZ = P.ps("psZ", [128, GS, 2, 128]); BpsZ = P.buf()
    psC = P.ps("psC", [128, GS, 2, 128]); BpsC = P.buf()
    psY = P.ps("psY", [128, 512]); BpsY = P.buf()
    psM = P.ps("psM", [128, 512]); BpsM = P.buf()
    hdn3 = P.sb("hdn3", [64, 2, 128, 64], BF16); Bhdn3 = P.buf()
    zts = [P.sb(f"zt{i}", [33, 512], F32) for i in range(2)]; Bzts = [P.buf() for _ in range(2)]
    us = [P.sb(f"u{i}", [64, 512], F32) for i in range(2)]; Bus = [P.buf() for _ in range(2)]
    hds = [P.sb(f"hd{i}", [64, 512], F32) for i in range(2)]; Bhds = [P.buf() for _ in range(2)]
    fr2 = P.sb("fr2", [64, 1], F32)
    uw = P.sb("uw", [64, 512], F32); Buw = P.buf()
    P.op("dve", lambda e: e.tensor_scalar(fr2[:], bfr[:, 3:4], 1.0 / TWO_PI, None, ALU.mult), reads=[Bw], writes=[Bw])
    cnt = 0
    for d in range(2):
        for i in range(L // 512):
            zt, Bzt = zts[cnt % 2], Bzts[cnt % 2]
            cnt += 1
            P.dma(zt[:], zf_d[:, d, i * 512:(i + 1) * 512], writes=[Bzt])
            src, Bsrc, kdim = zt, Bzt, 33
            for layer in range(3):
                wl = w1[:] if layer == 0 else w23[:, layer - 1, :]
                P.op("pe", lambda e, wl=wl, src=src, kdim=kdim: e.matmul(psM[0:64, :], wl, src[0:kdim, :], start=True, stop=True),
                     reads=[Bw, Bsrc], writes=[BpsM])
                u, Bu = us[layer % 2], Bus[layer % 2]
                P.op("dve", lambda e, u=u, layer=layer: e.tensor_scalar(u[:], psM[0:64, :], bfr[:, layer:layer + 1], fr2[:],
                                                                       ALU.add, ALU.mult), reads=[BpsM, Bw], writes=[Bu])
                for _w in range(3):
                    P.op("dve", lambda e, u=u: e.scalar_tensor_tensor(uw[:], u[:], -0.5, u[:], ALU.is_lt, ALU.add), reads=[Bu], writes=[Buw])
                    P.op("dve", lambda e, u=u: e.scalar_tensor_tensor(u[:], uw[:], 0.5, uw[:], ALU.is_gt, ALU.subtract), reads=[Buw], writes=[Bu])
                if layer < 2:
                    hd, Bhd = hds[layer % 2], Bhds[layer % 2]
                    P.op("act", lambda e, u=u, hd=hd: e.activation(hd[:], u[:], AF.Sin, scale=-TWO_PI * (1.0 - 1e-6)),
                         reads=[Bu, Bnegpi], writes=[Bhd])
                    src, Bsrc, kdim = hd, Bhd, 64
                else:
                    P.op("act", lambda e, u=u, d=d, i=i: e.activation(hdn3[:, d, :, 4 * i:4 * i + 4].rearrange("p b a -> p a b"), u[:].rearrange("p (a b) -> p a b", b=128), AF.Sin, scale=-TWO_PI * (1.0 - 1e-6)),
                         reads=[Bu, Bnegpi], writes=[Bhdn3])
    if stage <= 1:
        P.emit()
        return nc, P
    fraw = P.sb("fraw", [128, 128, 128], BF16); Bfraw = P.buf()
    dts = [P.sb(f"dct{i}", [128, 128, 4], F32) for i in range(2)]; Bdts = [P.buf() for _ in range(2)]
    hv = hdn3
    for g in range(32):
        dtile, Bdt = dts[g % 2], Bdts[g % 2]
        P.dma(dtile[:], decay_d[:, g, :, :], writes=[Bdt])
        for q in range(4):
            n2 = g * 4 + q
            P.op("pe", lambda e, q=q, n2=n2: e.matmul(psM[0:64, q * 128:(q + 1) * 128], hv[:, 0, n2, :], wout[:, 0, :],
                                                     start=True, stop=True), reads=[Bhdn3, Bwout], writes=[BpsM])
            P.op("pe", lambda e, q=q, n2=n2: e.matmul(psM[64:128, q * 128:(q + 1) * 128], hv[:, 1, n2, :], wout[:, 1, :],
                                                     start=True, stop=True), reads=[Bhdn3, Bwout], writes=[BpsM])
        P.op("dve", lambda e, g=g, dtile=dtile: e.tensor_tensor(
            fraw[:, :, g * 4:(g + 1) * 4], psM[:, :].rearrange("p (q c) -> p c q", q=4), dtile[:], ALU.mult),
            reads=[BpsM, Bdt], writes=[Bfraw])
    if stage <= 2:
        P.emit()
        return nc, P
    ssum = P.sb("ssum", [128, 128], F32); Bssum = P.buf()
    sqt = P.sb("sqt", [128, 16, 128], F32); Bsqt = P.buf()
    for cc in range(8):
        P.op("pool" if use_pool else "dve", lambda e, cc=cc: e.tensor_tensor(sqt[:], fraw[:, cc * 16:(cc + 1) * 16, :], fraw[:, cc * 16:(cc + 1) * 16, :], ALU.mult),
             reads=[Bfraw], writes=[Bsqt])
        P.op("dve", lambda e, cc=cc: e.tensor_reduce(ssum[:, cc * 16:(cc + 1) * 16], sqt[:], AX.X, ALU.add),
             reads=[Bsqt], writes=[Bssum])
    P.op("pe", lambda e: e.matmul(psM[:, 0:128], ones32[:], ssum[:], start=True, stop=True), reads=[Bones, Bssum], writes=[BpsM])
    rn = P.sb("rn", [128, 128], F32); Brn = P.buf()
    P.op("act", lambda e: e.activation(rn[:], psM[:, 0:128], AF.Ln, bias=epsb[:], scale=1.0), reads=[BpsM, Beps], writes=[Brn])
    P.op("act", lambda e: e.activation(rn[:], rn[:], AF.Exp, scale=-0.5), reads=[Brn], writes=[Brn])
    P.op("pool", lambda e: e.memset(fraw[64:65, :, 0:1], 0.0), reads=[Bssum], writes=[Bfraw])
    for cc in range(8):
        P.op("dve", lambda e, cc=cc: e.tensor_tensor(
            fraw[:, cc * 16:(cc + 1) * 16, :], fraw[:, cc * 16:(cc + 1) * 16, :],
            rn[:, cc * 16:(cc + 1) * 16].unsqueeze(2).broadcast_to([128, 16, 128]), ALU.mult),
            reads=[Bfraw, Brn], writes=[Bfraw])
    P.op("dve", lambda e: e.tensor_tensor(fraw[0:1, :, 0], fraw[0:1, :, 0], dbias[:], ALU.add), reads=[Bfraw, Bw], writes=[Bfraw])
    if stage <= 3:
        P.emit()
        return nc, P
    Bsb = P.sb("Bsb", [128, GS, 2, 128], BF16); BBsb = P.buf()
    Zsb = P.sb("Zsb", [128, GS, 2, 128], BF16); BZsb = P.buf()
    Ysb = P.sb("Ysb", [128, GS, 2, 128], BF16); BYsb = P.buf()
    Dsb = P.sb("Dsb", [128, 2, GS, 128], BF16); BDsb = P.buf()
    t1 = P.sb("t1", [128, GS, 128], F32); Bt1 = P.buf()
    t2 = P.sb("t2", [128, GS, 128], F32); Bt2 = P.buf()
    t3 = P.sb("t3", [128, GS, 128], F32); Bt3 = P.buf()
    t4 = P.sb("t4", [128, GS, 128], F32); Bt4 = P.buf()
    Hsb = P.sb("Hsb", [128, 128, 2, 128], BF16); BH = P.buf()
    twr = tw[:, 0, :].unsqueeze(1).broadcast_to([128, GS, 128])
    twi = tw[:, 1, :].unsqueeze(1).broadcast_to([128, GS, 128])

    def cmul(eng, dst_re, dst_im, a_re, a_im, b_re, b_im, conj, tA, tB, BtA, BtB, rd, wr):
        P.op(eng, lambda e: e.tensor_tensor(tA[:], a_re, b_re, ALU.mult), reads=rd, writes=[BtA])
        P.op(eng, lambda e: e.tensor_tensor(tB[:], a_im, b_im, ALU.mult), reads=rd, writes=[BtB])
        P.op(eng, lambda e: e.tensor_tensor(dst_re, tA[:], tB[:], ALU.add if conj else ALU.subtract),
             reads=[BtA, BtB], writes=wr)
        P.op(eng, lambda e: e.tensor_tensor(tA[:], a_im, b_re, ALU.mult), reads=rd, writes=[BtA])
        P.op(eng, lambda e: e.tensor_tensor(tB[:], a_re, b_im, ALU.mult), reads=rd, writes=[BtB])
        P.op(eng, lambda e: e.tensor_tensor(dst_im, tA[:], tB[:], ALU.subtract if conj else ALU.add),
             reads=[BtA, BtB], writes=wr)

    def fwd_fft(lhs_of, kdim, Bsrc):
        for s in range(GS):
            P.op("pe", lambda e, s=s: e.matmul(psA[:, s, :, :], lhs_of(s), con[0:kdim, 0:2, :], start=True, stop=True),
                 reads=[Bsrc, Bcon], writes=[BpsA])
        cmul("dve", Bsb[:, :, 0, :], Bsb[:, :, 1, :], psA[:, :, 0, :], psA[:, :, 1, :], twr, twi, False,
             t1, t2, Bt1, Bt2, [BpsA, Btw], [BBsb])
        for s0 in range(0, GS, 2):
            P.op("pe", lambda e, s0=s0: e.matmul(psZ[:, s0:s0 + 2, :, :], con[:, 0, :], Bsb[:, s0:s0 + 2, :, :],
                                                start=True, stop=False), reads=[BBsb, Bcon], writes=[BpsZ])
            P.op("pe", lambda e, s0=s0: e.matmul(psZ[:, s0:s0 + 2, 0, :], con[:, 3, :], Bsb[:, s0:s0 + 2, 1, :],
                                                start=False, stop=False), reads=[BBsb, Bcon], writes=[BpsZ])
            P.op("pe", lambda e, s0=s0: e.matmul(psZ[:, s0:s0 + 2, 1, :], con[:, 1, :], Bsb[:, s0:s0 + 2, 0, :],
                                                start=False, stop=True), reads=[BBsb, Bcon], writes=[BpsZ])

    for g in range(128 // GS):
        fwd_fft(lambda s, g=g: fraw[:, g * GS + s, :], 128, Bfraw)
        P.op("act", lambda e, g=g: e.activation(Hsb[:, g * GS:(g + 1) * GS, :, :], psZ[:], AF.Copy, scale=1.0 / NFFT),
             reads=[BpsZ], writes=[BH])
    if stage <= 4:
        P.emit()
        return nc, P
    zbs = [P.sb(f"zb{i}", [64, GS, 128], BF16) for i in range(3)]; Bzbs = [P.buf() for _ in range(3)]
    ysbs = [P.sb(f"yo{i}", [64, GS, 128], F32) for i in range(2)]; Bysbs = [P.buf() for _ in range(2)]
    for g in range(128):
        zb, Bzb = zbs[g % 3], Bzbs[g % 3]
        P.dma(zb[:], zin_d[:, g * GS:(g + 1) * GS, :], writes=[Bzb], eng="pool")
        fwd_fft(lambda s, zb=zb: zb[:, s, :], 64, Bzb)
        P.op("act", lambda e: e.activation(Zsb[:], psZ[:], AF.Copy), reads=[BpsZ], writes=[BZsb])
        hre = Hsb[:, g, 0, :].unsqueeze(1).broadcast_to([128, GS, 128])
        him = Hsb[:, g, 1, :].unsqueeze(1).broadcast_to([128, GS, 128])
        cmul("pool" if use_pool else "dve", Ysb[:, :, 0, :], Ysb[:, :, 1, :], Zsb[:, :, 0, :], Zsb[:, :, 1, :], hre, him, False,
             t3, t4, Bt3, Bt4, [BZsb, BH], [BYsb])
        for s in range(GS):
            P.op("pe", lambda e, s=s: e.matmul(psC[:, s, :, :], Ysb[:, s, 0, :], con[:, 2:4, :], start=True, stop=False),
                 reads=[BYsb, Bcon], writes=[BpsC])
            P.op("pe", lambda e, s=s: e.matmul(psC[:, s, :, :], Ysb[:, s, 1, :], con[:, 1:3, :], start=False, stop=True),
                 reads=[BYsb, Bcon], writes=[BpsC])
        cmul("dve", Dsb[:, 0, :, :], Dsb[:, 1, :, :], psC[:, :, 0, :], psC[:, :, 1, :], twr, twi, True,
             t1, t2, Bt1, Bt2, [BpsC, Btw], [BDsb])
        P.op("pe", lambda e: e.matmul(psY[0:64, :], con[:, 0, 0:64], Dsb[:, 0, :, :], start=True, stop=False),
             reads=[BDsb, Bcon], writes=[BpsY])
        P.op("pe", lambda e: e.matmul(psY[0:64, :], con[:, 1, 0:64], Dsb[:, 1, :, :], start=False, stop=True),
             reads=[BDsb, Bcon], writes=[BpsY])
        yo, Byo = ysbs[g % 2], Bysbs[g % 2]
        P.op("act", lambda e, yo=yo: e.activation(yo[:], psY[0:64, :], AF.Copy), reads=[BpsY], writes=[Byo])
        P.dma(y_d[:, g * GS:(g + 1) * GS, :], yo[:], reads=[Byo])
    P.emit()
    return nc, P


def _conv_constants():
    if "convc" in _NC_CACHE:
        return _NC_CACHE["convc"]
    f32 = np.float32
    t = np.linspace(0.0, 1.0, L, dtype=f32)[:, None]
    w = (f32(2.0 * math.pi / L) * np.arange(L, dtype=f32))[:, None]
    bands = np.linspace(1e-4, 15, 16, dtype=f32)[None, :]
    zfeat = np.concatenate([t, np.cos(bands * w), -np.sin(bands * w)], axis=-1).astype(f32)
    zf = np.empty((33, 2, L), f32)
    zf[:, 0, :] = zfeat.T
    idx = L - np.arange(L)
    idx[0] = 0
    zf[:, 1, :] = zfeat[idx].T
    j = np.arange(128)
    ang = 2.0 * np.pi * np.outer(j, j) / 128.0
    Fre, Fim = np.cos(ang), -np.sin(ang)
    con = np.stack([Fre, Fim, Fre, -Fim], axis=1).astype(f32)
    ang2 = 2.0 * np.pi * np.outer(j, j) / NFFT
    tw = np.stack([np.cos(ang2), -np.sin(ang2)], axis=1).astype(f32)
    deltas = np.abs(np.linspace(math.log(1e-2) / 0.3, math.log(1e-2) / 1.5, D, dtype=f32))
    p = np.arange(128)[:, None]
    n2 = np.arange(128)[None, :]
    m = np.where(p < 64, 128 * p + n2, L - (128 * (p - 64) + n2))
    m[64, 0] = 0
    tm = t[:, 0][m]
    _NC_CACHE["convc"] = (zf, con, tw, deltas, tm)
    return _NC_CACHE["convc"]


def run_conv(z, f_w1, f_b1, f_w2, f_b2, f_w3, f_b3, f_wout, f_freq, d_bias):
    nc, _ = _get_nc("conv", build_conv)
    zf, con, tw, deltas, tm = _conv_constants()
    shared = dict(zf=zf, con=con, tw=tw, w1=np.ascontiguousarray(f_w1),
                  w23=np.ascontiguousarray(np.stack([f_w2, f_w3], axis=1)),
                  bfr=np.ascontiguousarray(np.stack([f_b1, f_b2, f_b3, f_freq], axis=1)))
    in_maps = []
    for core in range(NCORES):
        ch0 = core * 128
        zc = z[:, :, ch0:ch0 + 128].reshape(NB, 64, 128, 128)
        zin = np.ascontiguousarray(zc.transpose(1, 3, 0, 2)).reshape(64, 512, 128)
        dec = np.exp(-tm[:, None, :] * deltas[None, ch0:ch0 + 128, None]).astype(np.float32)
        dec = np.ascontiguousarray(dec.reshape(128, 128, 32, 4).transpose(0, 2, 1, 3))
        wout = np.ascontiguousarray(np.stack([f_wout[:, ch0:ch0 + 128], f_wout[:, D + ch0:D + ch0 + 128]], axis=1))
        m = dict(shared)
        m.update(zin=zin, decay=dec, wout=wout, dbias=np.ascontiguousarray(d_bias[None, ch0:ch0 + 128]))
        in_maps.append(m)
    res = run_bass_kernel_spmd(nc, in_maps, core_ids=list(range(NCORES)))
    y = np.empty_like(z)
    for core in range(NCORES):
        ch0 = core * 128
        yc = res.results[core]["y"].reshape(64, 128, NB, 128)
        y[:, :, ch0:ch0 + 128] = yc.transpose(2, 0, 3, 1).reshape(NB, L, 128)
    return y


def build_hypre():
    nc = bass.Bass("TRN2", target_bir_lowering=False)
    xe = _dram(nc, "xe", [128, 8, TPC + 2])
    em_d = _dram(nc, "em", [128, 2])
    cT_d = _dram(nc, "cT", [128, 8, 1])
    wmod_d = _dram(nc, "wmod", [128, 8, 2 * D])
    bmod_d = _dram(nc, "bmod", [128, 2, 8])
    norm_d = _dram(nc, "norm", [128, 8])
    win_d = _dram(nc, "win", [128, 8, 3 * D])
    bin_d = _dram(nc, "bin", [128, 24])
    sw_d = _dram(nc, "sw", [128, 24, 3])
    sb_d = _dram(nc, "sb", [128, 24])
    z_d = _dram(nc, "z", [128, 8, TPC], kind="ExternalOutput")
    x0_d = _dram(nc, "x0", [128, 8, TPC], kind="ExternalOutput")
    P = Prog(nc)
    C = Common(P, nc)
    W = FW + 2
    win = P.sb("win", [128, 8, 3 * D], BF16); Bwin = [P.buf() for _ in range(8)]
    for k in range(8):
        P.dma(win[:, k, :], win_d[:, k, :], writes=[Bwin[k]], eng="pool")
    em = P.sb("em", [128, 2], F32); Bem = P.buf()
    nrm = P.sb("nrm", [128, 8], F32); Bnrm = P.buf()
    bi = P.sb("bi", [128, 24], F32); sw = P.sb("sw", [128, 24, 3], F32); sbb = P.sb("sbb", [128, 24], F32); Bcw = P.buf()
    P.dma(em[:], em_d, writes=[Bem]); P.dma(nrm[:], norm_d, writes=[Bnrm])
    P.dma(bi[:], bin_d, writes=[Bcw]); P.dma(sw[:], sw_d, writes=[Bcw]); P.dma(sbb[:], sb_d, writes=[Bcw])
    psN = P.ps("psN", [128, 512]); BpsN = P.buf()
    psG = [P.ps(f"psG{i}", [128, 512]) for i in range(2)]; BpsG = [P.buf() for _ in range(2)]
    mod, Bmod = C.mod_vectors(cT_d, wmod_d, bmod_d, 2, psN, BpsN)
    AA = P.sb("AA", [128, 8], F32); BAA = P.buf()
    P.op("dve", lambda e: e.tensor_scalar(AA[:], mod[:, 1, :, 0], 1.0, None, ALU.add), reads=[Bmod], writes=[BAA])
    P.op("dve", lambda e: e.tensor_tensor(AA[:], AA[:], nrm[:], ALU.mult), reads=[BAA, Bnrm], writes=[BAA])
    A = lambda k: AA[:, k:k + 1]
    Bsh = lambda k: mod[:, 0, k, :]
    BAB = P.buf()
    junk = P.sb("junk", [128, 1], F32)
    P.op("dve", lambda e: e.tensor_copy(junk[:], AA[:, 0:1]), reads=[BAA, Bmod], writes=[BAB])
    xts = [P.sb(f"xt{i}", [128, 8, W], F32) for i in range(2)]; Bxts = [P.buf() for _ in range(2)]
    sq = P.sb("sq", [128, 8, W], BF16); Bsq = P.buf()
    rstd = P.sb("rstd", [128, W], F32); Brstd = P.buf()
    tmps = [P.sb(f"tmp{i}", [128, W], F32) for i in range(2)]; Btmps = [P.buf() for _ in range(2)]
    ht = P.sb("ht", [128, 8, W], BF16); Bht = [P.buf() for _ in range(8)]
    Gs = [P.sb(f"G{i}", [128, W], F32) for i in range(2)]; BGs = [P.buf() for _ in range(2)]
    cvs = [P.sb(f"cv{i}", [128, FW], F32) for i in range(2)]; Bcvs = [P.buf() for _ in range(2)]
    x0s = [P.sb(f"x0s{i}", [128, 8, FW], F32) for i in range(2)]; Bx0s = [P.buf() for _ in range(2)]
    x1s = P.sb("x1s", [128, 8, FW], F32); Bx1s = [P.buf() for _ in range(8)]
    zs = [P.sb(f"zs{i}", [128, 8, FW], F32) for i in range(2)]; Bzs = [P.buf() for _ in range(2)]
    for t in range(NT_FFN):
        xt, Bxt = xts[t % 2], Bxts[t % 2]
        P.dma(xt[:], xe[:, :, FW * t:FW * t + W], writes=[Bxt])
        C.norm_mod(xt, Bxt, W, A, Bsh, BAB, sq, Bsq, psN, BpsN, rstd, Brstd, tmps, Btmps, ht, Bht)
        x0t, Bx0t = x0s[t % 2], Bx0s[t % 2]
        zt_, Bzt_ = zs[t % 2], Bzs[t % 2]
        for j in range(24):
            pg, Bpg = psG[j % 2], BpsG[j % 2]
            G, BG = Gs[j % 2], BGs[j % 2]
            for k in range(8):
                P.op("pe", lambda e, k=k, j=j, pg=pg: e.matmul(pg[:, 0:W], win[:, k, j * 128:(j + 1) * 128], ht[:, k, :],
                                                             start=(k == 0), stop=(k == 7)),
                     reads=[Bwin[k], Bht[k]], writes=[Bpg])
            P.op("act", lambda e, G=G, pg=pg, j=j: e.activation(G[:, 0:W], pg[:, 0:W], AF.Identity, bias=bi[:, j:j + 1], scale=1.0),
                 reads=[Bpg, Bcw], writes=[BG])
            if t == 0:
                P.op("dve", lambda e, G=G: e.tensor_scalar(G[:, 0:1], G[:, 0:1], em[:, 0:1], None, ALU.mult),
                     reads=[BG, Bem], writes=[BG])
            if t == NT_FFN - 1:
                P.op("dve", lambda e, G=G: e.tensor_scalar(G[:, W - 1:W], G[:, W - 1:W], em[:, 1:2], None, ALU.mult),
                     reads=[BG, Bem], writes=[BG])
            if j < 8:
                dst, Bdst = x0t[:, j, :], Bx0t
            elif j < 16:
                dst, Bdst = x1s[:, j - 8, :], Bx1s[j - 8]
            else:
                dst, Bdst = cvs[j % 2][:], Bcvs[j % 2]
            P.op("dve", lambda e, G=G, dst=dst, j=j: e.tensor_scalar(dst, G[:, 1:FW + 1], sw[:, j, 1:2], sbb[:, j:j + 1],
                                                                   ALU.mult, ALU.add), reads=[BG, Bcw], writes=[Bdst])
            P.op("dve", lambda e, G=G, dst=dst, j=j: e.scalar_tensor_tensor(dst, G[:, 0:FW], sw[:, j, 0:1], dst,
                                                                          ALU.mult, ALU.add), reads=[BG, Bcw, Bdst], writes=[Bdst])
            P.op("dve", lambda e, G=G, dst=dst, j=j: e.scalar_tensor_tensor(dst, G[:, 2:FW + 2], sw[:, j, 2:3], dst,
                                                                          ALU.mult, ALU.add), reads=[BG, Bcw, Bdst], writes=[Bdst])
            if j >= 16:
                P.op("pool", lambda e, dst=dst, j=j, zt_=zt_: e.tensor_tensor(zt_[:, j - 16, :], dst, x1s[:, j - 16, :], ALU.mult),
                     reads=[Bdst, Bx1s[j - 16]], writes=[Bzt_])
        P.dma(x0_d[:, :, FW * t:FW * (t + 1)], x0t[:], reads=[Bx0t])
        P.dma(z_d[:, :, FW * t:FW * (t + 1)], zt_[:], reads=[Bzt_])
    P.emit()
    return nc, P


def run_hypre(xin, c, w_mod, b_mod, norm1, w_in, b_in, short_w, short_b):
    nc, _ = _get_nc("hypre", build_hypre)
    shared = dict(
        wmod=pm(w_mod[:, 0:2 * D]),
        bmod=np.ascontiguousarray(b_mod[0:2 * D].reshape(2, 8, 128).transpose(2, 0, 1)),
        norm=vec_pm(norm1), win=pm(w_in), bin=vec_pm(b_in),
        sw=np.ascontiguousarray(short_w.reshape(3, 24, 128).transpose(2, 1, 0)), sb=vec_pm(short_b))
    in_maps = []
    for core in range(NCORES):
        b, hf = core // 2, core % 2
        t0 = hf * TPC
        xfm = feat_major(xin[b])
        em = np.zeros((128, 2), np.float32)
        em[:, 0] = 1.0 if t0 - 1 >= 0 else 0.0
        em[:, 1] = 1.0 if t0 + TPC < L else 0.0
        m = dict(shared)
        m.update(xe=_ext_cols(xfm, t0, TPC, 1), em=em, cT=vec_pm(c[b])[:, :, None].copy())
        in_maps.append(m)
    res = run_bass_kernel_spmd(nc, in_maps, core_ids=list(range(NCORES)))
    z = np.empty_like(xin)
    x0 = np.empty_like(xin)
    for core in range(NCORES):
        b, hf = core // 2, core % 2
        z[b, hf * TPC:(hf + 1) * TPC] = from_feat_major(res.results[core]["z"])
        x0[b, hf * TPC:(hf + 1) * TPC] = from_feat_major(res.results[core]["x0"])
    return z, x0


EW = 512


def build_hypost():
    nc = bass.Bass("TRN2", target_bir_lowering=False)
    y_d = _dram(nc, "y", [128, 8, TPC])
    x0_d = _dram(nc, "x0", [128, 8, TPC])
    x_d = _dram(nc, "x", [128, 8, TPC])
    cT_d = _dram(nc, "cT", [128, 8, 1])
    wmod_d = _dram(nc, "wmod", [128, 8, D])
    bmod_d = _dram(nc, "bmod", [128, 1, 8])
    wo_d = _dram(nc, "wo", [128, 8, D])
    bo_d = _dram(nc, "bo", [128, 8])
    out_d = _dram(nc, "out", [128, 8, TPC], kind="ExternalOutput")
    P = Prog(nc)
    wo = P.sb("wo", [128, 8, D], BF16); Bwo = P.buf()
    for k in range(8):
        P.dma(wo[:, k, :], wo_d[:, k, :], writes=[Bwo], eng="pool")
    bo = P.sb("bo", [128, 8], F32); Bbo = P.buf()
    P.dma(bo[:], bo_d, writes=[Bbo])
    psN = P.ps("psN", [128, 512]); BpsN = P.buf()
    psD = [P.ps(f"psD{i}", [128, 512]) for i in range(2)]; BpsD = [P.buf() for _ in range(2)]
    C = Common(P, nc)
    mod, Bmod = C.mod_vectors(cT_d, wmod_d, bmod_d, 1, psN, BpsN)
    gb = P.sb("gb", [128, 8], F32); Bgb = P.buf()
    P.op("dve", lambda e: e.tensor_tensor(gb[:], mod[:, 0, :, 0], bo[:], ALU.mult), reads=[Bmod, Bbo], writes=[Bgb])
    ys = [P.sb(f"ys{i}", [128, 8, EW], F32) for i in range(2)]; Bys = [P.buf() for _ in range(2)]
    x0s = [P.sb(f"x0s{i}", [128, 8, EW], F32) for i in range(2)]; Bx0s = [P.buf() for _ in range(2)]
    xs = [P.sb(f"xs{i}", [128, 8, EW], F32) for i in range(2)]; Bxs = [P.buf() for _ in range(2)]
    u = P.sb("u", [128, 8, EW], BF16); Bu = P.buf()
    for t in range(TPC // EW):
        yt, Byt = ys[t % 2], Bys[t % 2]
        x0t, Bx0t = x0s[t % 2], Bx0s[t % 2]
        xt, Bxt = xs[t % 2], Bxs[t % 2]
        sl = slice(EW * t, EW * (t + 1))
        P.dma(yt[:], y_d[:, :, sl], writes=[Byt])
        P.dma(x0t[:], x0_d[:, :, sl], writes=[Bx0t])
        P.dma(xt[:], x_d[:, :, sl], writes=[Bxt])
        P.op("dve", lambda e, yt=yt, x0t=x0t: e.tensor_tensor(u[:], x0t[:], yt[:], ALU.mult), reads=[Byt, Bx0t], writes=[Bu])
        for m in range(8):
            pd, Bpd = psD[m % 2], BpsD[m % 2]
            for k in range(8):
                P.op("pe", lambda e, k=k, m=m, pd=pd: e.matmul(pd[:], wo[:, k, m * 128:(m + 1) * 128], u[:, k, :],
                                                             start=(k == 0), stop=(k == 7)), reads=[Bwo, Bu], writes=[Bpd])
            P.op("dve", lambda e, m=m, pd=pd, xt=xt: e.scalar_tensor_tensor(xt[:, m, :], pd[:], mod[:, 0, m, :], xt[:, m, :],
                                                                          ALU.mult, ALU.add), reads=[Bpd, Bmod, Bxt], writes=[Bxt])
            P.op("act", lambda e, m=m, xt=xt: e.activation(xt[:, m, :], xt[:, m, :], AF.Identity, bias=gb[:, m:m + 1], scale=1.0),
                 reads=[Bxt, Bgb], writes=[Bxt])
        P.dma(out_d[:, :, sl], xt[:], reads=[Bxt])
    P.emit()
    return nc, P


def run_hypost(y, x0, xin, c, w_mod, b_mod, w_out, b_out):
    nc, _ = _get_nc("hypost", build_hypost)
    shared = dict(wmod=pm(w_mod[:, 2 * D:3 * D]),
                  bmod=np.ascontiguousarray(b_mod[2 * D:3 * D].reshape(1, 8, 128).transpose(2, 0, 1)),
                  wo=pm(w_out), bo=vec_pm(b_out))
    in_maps = []
    for core in range(NCORES):
        b, hf = core // 2, core % 2
        sl = slice(hf * TPC, (hf + 1) * TPC)
        m = dict(shared)
        m.update(y=feat_major(y[b, sl]), x0=feat_major(x0[b, sl]), x=feat_major(xin[b, sl]),
                 cT=vec_pm(c[b])[:, :, None].copy())
        in_maps.append(m)
    res = run_bass_kernel_spmd(nc, in_maps, core_ids=list(range(NCORES)))
    out = np.empty_like(xin)
    for core in range(NCORES):
        b, hf = core // 2, core % 2
        out[b, hf * TPC:(hf + 1) * TPC] = from_feat_major(res.results[core]["out"])
    return out


NEG = -30000.0
AQ = 512
AK = 1024
NJ = 23


def build_attn(nblk=8):
    nc = bass.Bass("TRN2", target_bir_lowering=False)
    xe = _dram(nc, "xe", [128, 8, 72 * 64])
    ctx_d = _dram(nc, "ctx", [128, 8, 256])
    cT_d = _dram(nc, "cT", [128, 8, 2])
    wmod_d = _dram(nc, "wmod", [128, 8, 3 * D])
    bmod_d = _dram(nc, "bmod", [128, 3, 8])
    norm_d = _dram(nc, "norm", [128, 8])
    wqkv_d = _dram(nc, "wqkv", [128, 8, 3 * D])
    wo_d = _dram(nc, "wo", [128, 8, D])
    gains_d = _dram(nc, "gains", [128, 2])
    cb_d = _dram(nc, "cb", [128, 16, NJ, 64])
    kaug_d = _dram(nc, "kaug", [16, 72 * 64])
    qaug_d = _dram(nc, "qaug", [16, TPC])
    ident_d = _dram(nc, "ident", [128, 2, 128])
    out_d = _dram(nc, "out", [128, 8, TPC], kind="ExternalOutput")
    P = Prog(nc)
    C = Common(P, nc)
    wqkv = P.sb("wqkv", [128, 8, 3 * D], BF16); Bwqkv = [P.buf() for _ in range(8)]
    for k in range(8):
        P.dma(wqkv[:, k, :], wqkv_d[:, k, :], writes=[Bwqkv[k]], eng="pool")
    wo = P.sb("wo", [128, 8, D], BF16); Bwo = P.buf()
    for k in range(8):
        P.dma(wo[:, k, :], wo_d[:, k, :], writes=[Bwo], eng="pool")
    idn = P.sb("idn", [128, 2, 128], BF16); Bidn = P.buf()
    P.dma(idn[:], ident_d, writes=[Bidn], eng="pool")
    kaug = P.sb("kaug", [16, 72 * 64], BF16); qaug = P.sb("qaug", [16, TPC], BF16); Baug = P.buf()
    P.dma(kaug[:], kaug_d, writes=[Baug], eng="pool")
    P.dma(qaug[:], qaug_d, writes=[Baug], eng="pool")
    nrm = P.sb("nrm", [128, 8], F32); Bnrm = P.buf()
    gains = P.sb("gains", [128, 2], F32); Bgains = P.buf()
    P.dma(nrm[:], norm_d, writes=[Bnrm]); P.dma(gains[:], gains_d, writes=[Bgains])
    P.op("dve", lambda e: e.tensor_scalar(gains[:, 0:1], gains[:, 0:1], 0.125, None, ALU.mult), reads=[Bgains], writes=[Bgains])
    psN = P.ps("psN", [128, 512]); BpsN = P.buf()
    psP = [P.ps(f"psP{i}", [128, 512]) for i in range(2)]; BpsP = [P.buf() for _ in range(2)]
    psS = [P.ps(f"psS{i}", [128, 512]) for i in range(2)]; BpsS = [P.buf() for _ in range(2)]
    psO = [P.ps(f"psO{i}", [128, 512]) for i in range(2)]; BpsO = [P.buf() for _ in range(2)]
    psV = P.ps("psV", [128, 512]); BpsV = P.buf()
    mod, Bmod = C.mod_vectors(cT_d, wmod_d, bmod_d, 3, psN, BpsN, ncol=2)
    AA = P.sb("AA", [128, 2, 8], F32); BAA = P.buf()
    for col in range(2):
        P.op("dve", lambda e, col=col: e.tensor_scalar(AA[:, col, :], mod[:, 1, :, col], 1.0, None, ALU.add), reads=[Bmod], writes=[BAA])
        P.op("dve", lambda e, col=col: e.tensor_tensor(AA[:, col, :], AA[:, col, :], nrm[:], ALU.mult), reads=[BAA, Bnrm], writes=[BAA])
    BAB = P.buf()
    junk = P.sb("junk", [128, 1], F32)
    P.op("dve", lambda e: e.tensor_copy(junk[:], AA[:, 0, 0:1]), reads=[BAA, Bmod], writes=[BAB])
    sq = P.sb("sq", [128, 8, 512], BF16); Bsq = P.buf()
    rstd = P.sb("rstd", [128, 512], F32); Brstd = P.buf()
    tmps = [P.sb(f"tmp{i}", [128, 512], F32) for i in range(2)]; Btmps = [P.buf() for _ in range(2)]
    sqh = P.sb("sqh", [128, 512], BF16); Bsqh = P.buf()
    rsh = P.sb("rsh", [128, 512], F32); Brsh = P.buf()

    def head_norm(psrc, Bpsrc, dst, Bdst, W, gcol):
        P.op("act", lambda e: e.activation(sqh[:, 0:W], psrc[:, 0:W], AF.Square), reads=[Bpsrc], writes=[Bsqh])
        P.op("pe", lambda e: e.matmul(psN[:, 0:W], idn[:, 1, :], sqh[:, 0:W], start=True, stop=True),
             reads=[Bidn, Bsqh], writes=[BpsN])
        P.op("act", lambda e: e.activation(rsh[:, 0:W], psN[:, 0:W], AF.Ln, bias=C.epsb[:], scale=1.0),
             reads=[BpsN, C.Beps], writes=[Brsh])
        P.op("act", lambda e: e.activation(rsh[:, 0:W], rsh[:, 0:W], AF.Exp, scale=-0.5), reads=[Brsh], writes=[Brsh])
        P.op("dve", lambda e: e.scalar_tensor_tensor(dst, psrc[:, 0:W], gains[:, gcol:gcol + 1], rsh[:, 0:W], ALU.mult, ALU.mult),
             reads=[Bpsrc, Brsh, Bgains], writes=[Bdst])

    xt = P.sb("xt", [128, 8, AK], F32); Bxt = P.buf()
    cx, Bcx = xt, Bxt
    P.dma(cx[:, :, 0:256], ctx_d, writes=[Bcx])
    hc = P.sb("hc", [128, 8, 256], BF16); Bhc = [P.buf() for _ in range(8)]
    C.norm_mod(cx, Bcx, 256, lambda k: AA[:, 1, k:k + 1], lambda k: mod[:, 0, k, 1:2], BAB, sq, Bsq, psN, BpsN,
               rstd, Brstd, tmps, Btmps, hc, Bhc)
    kcT = P.sb("kcT", [128, 8, 256], BF16); BkcT = P.buf()
    VCA = P.sb("VCA", [128, 2, 8, 2, 128], BF16); BVCA = P.buf()
    P.op("pool", lambda e: e.memset(VCA[:], 1.0), writes=[BVCA])
    for hp in range(8):
        pp, Bpp = psP[hp % 2], BpsP[hp % 2]
        for k in range(8):
            P.op("pe", lambda e, k=k, hp=hp, pp=pp: e.matmul(pp[:, 0:256], wqkv[:, k, D + hp * 128:D + (hp + 1) * 128], hc[:, k, :],
                                                           start=(k == 0), stop=(k == 7)), reads=[Bwqkv[k], Bhc[k]], writes=[Bpp])
        head_norm(pp, Bpp, kcT[:, hp, :], BkcT, 256, 1)
        for cc in range(2):
            for k in range(8):
                P.op("pe", lambda e, k=k, hp=hp, cc=cc: e.matmul(psV[:, 0:128], hc[:, k, cc * 128:(cc + 1) * 128],
                                                               wqkv[:, k, 2 * D + hp * 128:2 * D + (hp + 1) * 128],
                                                               start=(k == 0), stop=(k == 7)), reads=[Bwqkv[k], Bhc[k]], writes=[BpsV])
            P.op("act", lambda e, hp=hp, cc=cc: e.activation(VCA[:, cc, hp, 0, 0:64], psV[:, 0:64], AF.Copy), reads=[BpsV], writes=[BVCA])
            P.op("dve", lambda e, hp=hp, cc=cc: e.tensor_copy(VCA[:, cc, hp, 1, 64:128], psV[:, 64:128]), reads=[BpsV], writes=[BVCA])
    ht = P.sb("ht", [128, 8, AK], BF16); Bht = [P.buf() for _ in range(8)]
    qT = P.sb("qT", [128, AQ], BF16); BqT = P.buf()
    kT = P.sb("kT", [128, AK], BF16); BkT = P.buf()
    VA = P.sb("VA", [128, 8, 2, 128], BF16); BVA = P.buf()
    P.op("pool", lambda e: e.memset(VA[:], 1.0), writes=[BVA])
    PTs = [P.sb(f"PT{i}", [128, AQ], BF16) for i in range(3)]; BPTs = [P.buf() for _ in range(3)]
    rec = P.sb("rec", [128, AQ], F32); Brec = P.buf()
    attn = P.sb("attn", [128, 8, AQ], BF16); Battn = [P.buf() for _ in range(8)]
    cbs = [P.sb(f"cbs{i}", [128, 2, NJ, 64], BF16) for i in range(2)]; Bcbs = [P.buf() for _ in range(2)]
    ptc = 0
    cbc = 0
    for qb in range(nblk):
        P.dma(xt[:], xe[:, :, 512 * qb:512 * qb + AK], writes=[Bxt])
        for half in range(2):
            C.norm_mod(xt, Bxt, 512, lambda k: AA[:, 0, k:k + 1], lambda k: mod[:, 0, k, 0:1], BAB, sq, Bsq, psN, BpsN,
                       rstd, Brstd, tmps, Btmps, ht, Bht, c0=512 * half)
        for hp in range(8):
            cbt, Bcbt = cbs[cbc % 2], Bcbs[cbc % 2]
            cbc += 1
            P.dma(cbt[:], cb_d[:, 2 * hp:2 * hp + 2, :, :], writes=[Bcbt], eng="pool")
            pp, Bpp = psP[0], BpsP[0]
            for k in range(8):
                P.op("pe", lambda e, k=k, hp=hp, pp=pp: e.matmul(pp[:], wqkv[:, k, hp * 128:(hp + 1) * 128], ht[:, k, 256:768],
                                                               start=(k == 0), stop=(k == 7)), reads=[Bwqkv[k], Bht[k]], writes=[Bpp])
            head_norm(pp, Bpp, qT[:], BqT, 512, 0)
            for half in range(2):
                pp, Bpp = psP[1], BpsP[1]
                for k in range(8):
                    P.op("pe", lambda e, k=k, hp=hp, pp=pp, half=half: e.matmul(
                        pp[:], wqkv[:, k, D + hp * 128:D + (hp + 1) * 128], ht[:, k, 512 * half:512 * (half + 1)],
                        start=(k == 0), stop=(k == 7)), reads=[Bwqkv[k], Bht[k]], writes=[Bpp])
                head_norm(pp, Bpp, kT[:, 512 * half:512 * (half + 1)], BkT, 512, 1)
            for ci in range(8):
                for k in range(8):
                    P.op("pe", lambda e, k=k, hp=hp, ci=ci: e.matmul(psV[:, 0:128], ht[:, k, ci * 128:(ci + 1) * 128],
                                                                   wqkv[:, k, 2 * D + hp * 128:2 * D + (hp + 1) * 128],
                                                                   start=(k == 0), stop=(k == 7)), reads=[Bwqkv[k], Bht[k]], writes=[BpsV])
                P.op("act", lambda e, ci=ci: e.activation(VA[:, ci, 0, 0:64], psV[:, 0:64], AF.Copy), reads=[BpsV], writes=[BVA])
                P.op("dve", lambda e, ci=ci: e.tensor_copy(VA[:, ci, 1, 64:128], psV[:, 64:128]), reads=[BpsV], writes=[BVA])
            for hh in range(2):
                r0, r1 = 64 * hh, 64 * hh + 64
                po, Bpo = psO[hh], BpsO[hh]
                for i in range(10):
                    ps_, Bps_ = psS[i % 2], BpsS[i % 2]
                    PT, BPT = PTs[ptc % 3], BPTs[ptc % 3]
                    ptc += 1
                    if i < 8:
                        P.op("pe", lambda e, i=i, ps_=ps_, r0=r0, r1=r1: e.matmul(ps_[:], kT[r0:r1, i * 128:(i + 1) * 128], qT[r0:r1, :],
                                                                              start=True, stop=False), reads=[BkT, BqT], writes=[Bps_])
                        j0 = 15 - 2 * i
                        P.op("pe", lambda e, ps_=ps_, cbt=cbt, hh=hh, j0=j0: e.matmul(ps_[:], idn[:, 0, :], cbt[:, hh, j0:j0 + 8, :],
                                                                                  start=False, stop=False), reads=[Bidn, Bcbt], writes=[Bps_])
                        P.op("pe", lambda e, ps_=ps_, i=i, qb=qb: e.matmul(ps_[:], kaug[:, 512 * qb + i * 128:512 * qb + (i + 1) * 128],
                                                                       qaug[:, 512 * qb:512 * (qb + 1)], start=False, stop=True),
                             reads=[Baug], writes=[Bps_])
                        lhsV = VA[:, i, hh, :]
                        BV = BVA
                    else:
                        cc = i - 8
                        P.op("pe", lambda e, cc=cc, ps_=ps_, r0=r0, r1=r1, hp=hp: e.matmul(ps_[:], kcT[r0:r1, hp, cc * 128:(cc + 1) * 128],
                                                                                       qT[r0:r1, :], start=True, stop=True),
                             reads=[BkcT, BqT], writes=[Bps_])
                        lhsV = VCA[:, cc, hp, hh, :]
                        BV = BVCA
                    P.op("act", lambda e, PT=PT, ps_=ps_: e.activation(PT[:], ps_[:], AF.Exp), reads=[Bps_], writes=[BPT])
                    P.op("pe", lambda e, po=po, lhsV=lhsV, PT=PT, i=i: e.matmul(po[:], lhsV, PT[:], start=(i == 0), stop=(i == 9)),
                         reads=[BV, BPT], writes=[Bpo])
                d0, d1 = 64 - 64 * hh, 128 - 64 * hh
                P.op("dve", lambda e, po=po, r0=r0, r1=r1, d0=d0, d1=d1: e.reciprocal(rec[r0:r1, :], po[d0:d1, :]),
                     reads=[Bpo], writes=[Brec])
                P.op("dve", lambda e, po=po, r0=r0, r1=r1, hp=hp: e.tensor_tensor(attn[r0:r1, hp, :], po[r0:r1, :], rec[r0:r1, :], ALU.mult),
                     reads=[Bpo, Brec], writes=[Battn[hp]])
        for m in range(8):
            pp, Bpp = psP[m % 2], BpsP[m % 2]
            for hp in range(8):
                P.op("pe", lambda e, m=m, hp=hp, pp=pp: e.matmul(pp[:], wo[:, hp, m * 128:(m + 1) * 128], attn[:, hp, :],
                                                               start=(hp == 0), stop=(hp == 7)), reads=[Bwo, Battn[hp]], writes=[Bpp])
            P.op("dve", lambda e, m=m, pp=pp: e.scalar_tensor_tensor(xt[:, m, 256:768], pp[:], mod[:, 2, m, 0:1], xt[:, m, 256:768],
                                                                   ALU.mult, ALU.add), reads=[Bpp, Bmod, Bxt], writes=[Bxt])
        P.dma(out_d[:, :, 512 * qb:512 * (qb + 1)], xt[:, :, 256:768], reads=[Bxt])
    P.emit()
    return nc, P


def _attn_constants():
    if "attnc" in _NC_CACHE:
        return _NC_CACHE["attnc"]
    ident = np.zeros((128, 2, 128), np.float32)
    ident[:, 0, :] = np.eye(128, dtype=np.float32)
    ident[0:64, 1, 0:64] = 1.0 / 64
    ident[64:128, 1, 64:128] = 1.0 / 64
    augs = []
    for hf in range(2):
        Q0 = hf * 64
        e = np.arange(72 * 64)
        krow = Q0 - 4 + e // 64
        kaug = (np.mod(krow, 16)[None, :] == np.arange(16)[:, None]).astype(np.float32)
        q = np.arange(TPC)
        qr = Q0 + q // 64
        R = Q0 + 8 * (q // 512)
        a = np.arange(16)[:, None]
        kr = (R - 4)[None, :] + np.mod(a - (R - 4)[None, :], 16)
        r0 = np.clip(qr - 4, 0, 120)[None, :]
        valid = (kr >= r0) & (kr <= r0 + 7)
        qaug = np.where(valid, 0.0, NEG).astype(np.float32)
        augs.append((kaug, qaug))
    kc = np.arange(64)[:, None, None]
    jj = np.arange(NJ)[None, :, None]
    qc = np.arange(64)[None, None, :]
    c0 = np.clip(qc - 8, 0, 48)
    colok = (kc >= c0) & (kc <= c0 + 15)
    dc = np.clip(kc - qc + 15, 0, 30)
    tabs = []
    for krl in range(2):
        j = jj - krl - 4
        rowok = (j >= 0) & (j <= 14)
        dr = np.clip(14 - j, 0, 14)
        tabs.append((np.broadcast_to(rowok, (64, NJ, 64)).copy(), np.broadcast_to(colok, (64, NJ, 64)).copy(),
                     np.broadcast_to(dr, (64, NJ, 64)).copy(), np.broadcast_to(dc, (64, NJ, 64)).copy()))
    _NC_CACHE["attnc"] = (ident, augs, tabs)
    return _NC_CACHE["attnc"]


def run_attn(x, c, ctx, c_ctx, w_mod, b_mod, norm1, w_qkv, q_gain, k_gain, rpb, w_o):
    nc, _ = _get_nc("attn", build_attn)
    ident, augs, tabs = _attn_constants()
    cb = np.empty((128, 16, NJ, 64), np.float32)
    for krl in range(2):
        rowok, colok, dr, dc = tabs[krl]
        g = rpb[:, dr, dc]
        g = np.where(colok[None], g, np.float32(NEG))
        g = np.where(rowok[None], g, np.float32(0.0))
        cb[krl * 64:(krl + 1) * 64] = g.transpose(1, 0, 2, 3)
    shared = dict(wmod=pm(w_mod[:, 0:3 * D]),
                  bmod=np.ascontiguousarray(b_mod[0:3 * D].reshape(3, 8, 128).transpose(2, 0, 1)),
                  norm=vec_pm(norm1), wqkv=pm(w_qkv), wo=pm(w_o),
                  gains=np.ascontiguousarray(np.stack([np.tile(q_gain, 2), np.tile(k_gain, 2)], axis=1)),
                  cb=cb, ident=ident)
    in_maps = []
    for core in range(NCORES):
        b, hf = core // 2, core % 2
        t0 = hf * TPC
        xfm = feat_major(x[b])
        m = dict(shared)
        m.update(xe=_ext_cols(xfm, t0, TPC, 256), ctx=feat_major(ctx[b]),
                 cT=np.ascontiguousarray(np.stack([vec_pm(c[b]), vec_pm(c_ctx)], axis=2)),
                 kaug=augs[hf][0], qaug=augs[hf][1])
        in_maps.append(m)
    res = run_bass_kernel_spmd(nc, in_maps, core_ids=list(range(NCORES)))
    out = np.empty_like(x)
    for core in range(NCORES):
        b, hf = core // 2, core % 2
        out[b, hf * TPC:(hf + 1) * TPC] = from_feat_major(res.results[core]["out"])
    return out


def kernel(**inp):
    a = {k: np.ascontiguousarray(np.asarray(v, dtype=np.float32)) for k, v in inp.items()}
    x, c, ctx, c_ctx = a["x"], a["c"], a["ctx"], a["c_ctx"]
    x1 = run_attn(x, c, ctx, c_ctx, a["l0_w_mod"], a["l0_b_mod"], a["l0_norm1"], a["l0_na_w_qkv"],
                  a["l0_na_q_gain"], a["l0_na_k_gain"], a["l0_na_rpb"], a["l0_na_w_o"])
    x2 = run_ffn(x1, c, a["l0_w_mod"], a["l0_b_mod"], a["l0_norm2"], a["l0_ffn_w_up"], a["l0_ffn_conv_w"],
                 a["l0_ffn_conv_b"], a["l0_ffn_w_down"])
    z, x0 = run_hypre(x2, c, a["l1_w_mod"], a["l1_b_mod"], a["l1_norm1"], a["l1_hy_w_in"], a["l1_hy_b_in"],
                      a["l1_hy_short_w"], a["l1_hy_short_b"])
    y = run_conv(z, a["l1_hy_f_w1"], a["l1_hy_f_b1"], a["l1_hy_f_w2"], a["l1_hy_f_b2"], a["l1_hy_f_w3"],
                 a["l1_hy_f_b3"], a["l1_hy_f_wout"], a["l1_hy_f_freq"], a["l1_hy_d_bias"])
    x25 = run_hypost(y, x0, x2, c, a["l1_w_mod"], a["l1_b_mod"], a["l1_hy_w_out"], a["l1_hy_b_out"])
    x3 = run_ffn(x25, c, a["l1_w_mod"], a["l1_b_mod"], a["l1_norm2"], a["l1_ffn_w_up"], a["l1_ffn_conv_w"],
                 a["l1_ffn_conv_b"], a["l1_ffn_w_down"])
    return x3.astype(np.float32)
```
